# Optimizing a Trainium2 kernel written in Bass

```python
import jax, jax.numpy as jnp
from jax import lax
import numpy as np

D_MODEL = 4096
BATCH = 2
SEQ = 4096
DEPTH = 1

CHUNK = 64
LEFT_CHUNKS = 8
BAND = LEFT_CHUNKS + 1
HEAD_DIM = 128
N_HEADS_A = 16
N_HEADS_B = 16
WIDTH_A = N_HEADS_A * HEAD_DIM
WIDTH_B = N_HEADS_B * HEAD_DIM
MAX_REL = 256
N_REL = 2 * MAX_REL + 1
Q_BLOCK = 128
N_BRANCHES = 2
D_FF = -(-(8 * D_MODEL) // (3 * 256)) * 256
D_IN = 3 * WIDTH_A + 3 * WIDTH_B + N_HEADS_B + N_BRANCHES * D_MODEL
RMS_EPS = 1e-6
NEG_INF = -1e30

kernel_name = "hybrid_chunked_relpos_fox_gated_block"


def rms_norm(x, g):
    xf = x.astype(jnp.float32)
    y = xf * lax.rsqrt(jnp.mean(xf * xf, axis=-1, keepdims=True) + RMS_EPS)
    return (y * g.astype(jnp.float32)).astype(x.dtype)


def chunked_relpos_attention(q, k, v, rel_bias):
    b, s, h, dh = q.shape
    n_chunks = s // CHUNK
    q = q.reshape(b, n_chunks, CHUNK, h, dh)
    pad = ((0, 0), (LEFT_CHUNKS, 0), (0, 0), (0, 0), (0, 0))
    kp = jnp.pad(k.reshape(b, n_chunks, CHUNK, h, dh), pad)
    vp = jnp.pad(v.reshape(b, n_chunks, CHUNK, h, dh), pad)
    band_idx = jnp.arange(n_chunks)[:, None] + jnp.arange(BAND)[None, :]
    k_band = kp[:, band_idx].reshape(b, n_chunks, BAND * CHUNK, h, dh)
    v_band = vp[:, band_idx].reshape(b, n_chunks, BAND * CHUNK, h, dh)
    valid = (band_idx - LEFT_CHUNKS) >= 0
    valid = jnp.repeat(valid, CHUNK, axis=1)
    a_pos = jnp.arange(CHUNK)[:, None, None]
    slot = jnp.arange(BAND)[None, :, None]
    b_pos = jnp.arange(CHUNK)[None, None, :]
    dist = ((LEFT_CHUNKS - slot) * CHUNK + a_pos - b_pos).reshape(CHUNK, BAND * CHUNK)
    rel_idx = jnp.clip(dist, -MAX_REL, MAX_REL) + MAX_REL
    bias = rel_bias.astype(jnp.float32)[:, rel_idx]
    scale = HEAD_DIM ** -0.5
    logits = jnp.einsum('bcqhd,bckhd->bhcqk', q, k_band,
                        preferred_element_type=jnp.float32) * scale
    logits = logits + bias[None, :, None, :, :]
    logits = jnp.where(valid[None, None, :, None, :], logits, NEG_INF)
    p = jax.nn.softmax(logits, axis=-1).astype(v.dtype)
    out = jnp.einsum('bhcqk,bckhd->bcqhd', p, v_band)
    return out.reshape(b, s, h * dh)


def forgetting_attention(q, k, v, log_f):
    b, s, h, dh = q.shape
    n_blocks = s // Q_BLOCK
    cum = jnp.cumsum(log_f, axis=1)
    cum_k = cum.transpose(0, 2, 1)
    q_blocks = q.reshape(b, n_blocks, Q_BLOCK, h, dh).transpose(1, 0, 2, 3, 4)
    c_blocks = cum.reshape(b, n_blocks, Q_BLOCK, h).transpose(1, 0, 3, 2)
    starts = jnp.arange(n_blocks) * Q_BLOCK
    k_pos = jnp.arange(s)
    scale = HEAD_DIM ** -0.5

    def one_block(args):
        q_i, c_i, start = args
        logits = jnp.einsum('bqhd,bkhd->bhqk', q_i, k,
                            preferred_element_type=jnp.float32) * scale
        logits = logits + c_i[..., :, None] - cum_k[:, :, None, :]
        q_pos = start + jnp.arange(Q_BLOCK)
        causal = k_pos[None, :] <= q_pos[:, None]
        logits = jnp.where(causal[None, None], logits, NEG_INF)
        p = jax.nn.softmax(logits, axis=-1).astype(v.dtype)
        return jnp.einsum('bhqk,bkhd->bqhd', p, v)

    out = lax.map(one_block, (q_blocks, c_blocks, starts))
    return out.transpose(1, 0, 2, 3, 4).reshape(b, s, h * dh)


def setup_inputs(seed: int = 0) -> dict:
    key = jax.random.key(seed)
    ks = jax.random.split(key, 16)
    L = DEPTH

    def dense(k, shape, fan_in):
        return jax.random.normal(k, shape, jnp.float32) * fan_in ** -0.5

    def normal(k, shape):
        return jax.random.normal(k, shape, jnp.float32)

    return {
        "x": normal(ks[0], (BATCH, SEQ, D_MODEL)),
        "g_mix": 1.0 + 0.02 * normal(ks[1], (L, D_MODEL)),
        "w_in": dense(ks[2], (L, D_MODEL, D_IN), D_MODEL),
        "b_f": 2.0 + 0.5 * normal(ks[3], (L, N_HEADS_B)),
        "b_gate": 0.01 * normal(ks[4], (L, N_BRANCHES * D_MODEL)),
        "rel_bias": 0.1 * normal(ks[5], (L, N_HEADS_A, N_REL)),
        "w_branch_a": dense(ks[6], (L, WIDTH_A, D_MODEL), WIDTH_A),
        "w_branch_b": dense(ks[7], (L, WIDTH_B, D_MODEL), WIDTH_B),
        "w_out": dense(ks[8], (L, D_MODEL, D_MODEL), D_MODEL),
        "g_ffn": 1.0 + 0.02 * normal(ks[9], (L, D_MODEL)),
        "w_gate_ffn": dense(ks[10], (L, D_MODEL, D_FF), D_MODEL),
        "w_up_ffn": dense(ks[11], (L, D_MODEL, D_FF), D_MODEL),
        "w_down_ffn": dense(ks[12], (L, D_FF, D_MODEL), D_FF),
        "g_final": 1.0 + 0.02 * normal(ks[13], (D_MODEL,)),
    }


def reference(x, g_mix, w_in, b_f, b_gate, rel_bias, w_branch_a, w_branch_b, w_out,
              g_ffn, w_gate_ffn, w_up_ffn, w_down_ffn, g_final):
    b, s, _ = x.shape
    sizes = [WIDTH_A, WIDTH_A, WIDTH_A, WIDTH_B, WIDTH_B, WIDTH_B, N_HEADS_B, D_MODEL, D_MODEL]
    split_at = [int(v) for v in np.cumsum(sizes)[:-1]]
    for l in range(DEPTH):
        h = rms_norm(x, g_mix[l])
        proj = jnp.einsum('bsd,de->bse', h, w_in[l])
        qa, ka, va, qb, kb, vb, f_logit, gate_a, gate_b = jnp.split(proj, split_at, axis=-1)
        heads_a = lambda t: t.reshape(b, s, N_HEADS_A, HEAD_DIM)
        heads_b = lambda t: t.reshape(b, s, N_HEADS_B, HEAD_DIM)
        o_a = chunked_relpos_attention(heads_a(qa), heads_a(ka), heads_a(va), rel_bias[l])
        log_f = jax.nn.log_sigmoid((f_logit + b_f[l]).astype(jnp.float32))
        o_b = forgetting_attention(heads_b(qb), heads_b(kb), heads_b(vb), log_f)
        u_a = jnp.einsum('bse,ed->bsd', o_a, w_branch_a[l])
        u_b = jnp.einsum('bse,ed->bsd', o_b, w_branch_b[l])
        merged = (jax.nn.sigmoid(gate_a + b_gate[l, :D_MODEL]) * u_a
                  + jax.nn.sigmoid(gate_b + b_gate[l, D_MODEL:]) * u_b)
        x = x + jnp.einsum('bsd,de->bse', merged, w_out[l])
        h2 = rms_norm(x, g_ffn[l])
        hidden = jax.nn.silu(jnp.einsum('bsd,df->bsf', h2, w_gate_ffn[l])) * \
            jnp.einsum('bsd,df->bsf', h2, w_up_ffn[l])
        x = x + jnp.einsum('bsf,fd->bsd', hidden, w_down_ffn[l])
    return rms_norm(x, g_final)
```

```python
import contextlib
import numpy as np
import concourse.bass as bass
import concourse.mybir as mybir
from concourse.bass_utils import run_bass_kernel_spmd

F32 = mybir.dt.float32
BF16 = mybir.dt.bfloat16
AF = mybir.ActivationFunctionType
ALU = mybir.AluOpType
AX = mybir.AxisListType

D = 4096
T = 1024
WIN = 4096
DIN = 20496
DFF = 11008
NH = 16
HD = 128
QA, KA, VA, QB, KB, VB, FC, GA, GB = 0, 2048, 4096, 6144, 8192, 10240, 12288, 12304, 16400
NEG = -30000.0
EPS = 1e-6
ENGS = ["sync", "scalar", "vector", "gpsimd", "tensor"]


class Op:
    __slots__ = ("eng", "fn", "deps", "is_dma", "need_inc", "cnt", "dsem", "dval")

    def __init__(self, eng, fn, is_dma):
        self.eng = eng
        self.fn = fn
        self.deps = ()
        self.is_dma = is_dma
        self.need_inc = False
        self.cnt = 0
        self.dsem = None
        self.dval = 0


class Sched:
    def __init__(self, nc, ndma=8):
        self.nc = nc
        self.ops = {e: [] for e in ENGS}
        self.lastw = {}
        self.readers = {}
        self.ndma = ndma
        self.dmas = []

    def add(self, eng, fn, reads=(), writes=(), dma=False):
        op = Op(eng, fn, dma)
        deps = set()
        for b in reads:
            w = self.lastw.get(b)
            if w is not None:
                deps.add(w)
        for b in writes:
            w = self.lastw.get(b)
            if w is not None:
                deps.add(w)
            r = self.readers.get(b)
            if r:
                deps.update(r)
        for b in reads:
            lst = self.readers.setdefault(b, [])
            if not dma:
                for i, o in enumerate(lst):
                    if (not o.is_dma) and o.eng == eng:
                        lst[i] = op
                        break
                else:
                    lst.append(op)
            else:
                lst.append(op)
        for b in writes:
            self.lastw[b] = op
            self.readers[b] = []
        deps.discard(op)
        op.deps = deps
        self.ops[eng].append(op)
        if dma:
            self.dmas.append(op)
        return op

    def dma(self, eng, out, in_, reads=(), writes=()):
        return self.add(eng, lambda e: e.dma_start(out=out, in_=in_), reads, writes, dma=True)

    def barrier(self):
        deps = set(self.dmas)
        self.dmas = []
        for e in ENGS:
            for o in reversed(self.ops[e]):
                if (not o.is_dma) and o.fn is not None:
                    deps.add(o)
                    break
        for e in ENGS:
            op = Op(e, None, False)
            op.deps = set(deps)
            self.ops[e].append(op)
        self.lastw.clear()
        self.readers.clear()

    def emit(self):
        nc = self.nc
        for e in ENGS:
            for op in self.ops[e]:
                for d in op.deps:
                    if d.is_dma:
                        continue
                    if d.eng == "tensor" and op.eng == "tensor" and not op.is_dma:
                        continue
                    d.need_inc = True
        for e in ENGS:
            c = 0
            for op in self.ops[e]:
                if op.is_dma:
                    continue
                if op.need_inc:
                    c += 1
                    op.cnt = c
        with contextlib.ExitStack() as st:
            esem = {e: st.enter_context(nc.semaphore("s_" + e)) for e in ENGS}
            dsems = {}
            for e in ("sync", "scalar", "gpsimd"):
                if any(o.is_dma for o in self.ops[e]):
                    dsems[e] = [st.enter_context(nc.semaphore("d_%s%d" % (e, i))) for i in range(self.ndma)]
            for e in dsems:
                k = 0
                cnts = [0] * self.ndma
                for op in self.ops[e]:
                    if op.is_dma:
                        s = k % self.ndma
                        cnts[s] += 16
                        op.dsem = dsems[e][s]
                        op.dval = cnts[s]
                        k += 1
            block = st.enter_context(nc.Block())
            stats = {}

            def body_for(ename):
                def body(eng):
                    waited = {}
                    nw = 0
                    for op in self.ops[ename]:
                        waits = {}
                        for d in op.deps:
                            if d.is_dma:
                                key, val = d.dsem, d.dval
                            else:
                                if d.eng == "tensor" and ename == "tensor" and not op.is_dma:
                                    continue
                                key, val = esem[d.eng], d.cnt
                            if waits.get(key, 0) < val:
                                waits[key] = val
                        if op.is_dma and op.dval > 16:
                            key, val = op.dsem, op.dval - 16
                            if waits.get(key, 0) < val:
                                waits[key] = val
                        for key, val in waits.items():
                            if waited.get(id(key), 0) >= val:
                                continue
                            waited[id(key)] = val
                            eng.wait_ge(key, val)
                            nw += 1
                        if op.fn is None:
                            continue
                        ins = op.fn(eng)
                        if op.is_dma:
                            ins.then_inc(op.dsem, 16)
                        elif op.need_inc:
                            ins.then_inc(esem[ename], 1)
                    stats[ename] = (len(self.ops[ename]), nw)
                return body

            for e in ENGS:
                if self.ops[e]:
                    getattr(block, e)(body_for(e))
            self.stats = stats


def build(debug=None, upto=99, ext_in=(), phases=None):
    debug = debug or set()
    ext_in = set(ext_in)
    if phases is None:
        phases = {1, 15, 2, 3, 4, 5, 6}
    phases = set(phases)
    nc = bass.Bass("TRN2", target_bir_lowering=False)

    def dram_in(name, shape, dt=F32):
        return nc.dram_tensor(name, list(shape), dt, kind="ExternalInput").ap()

    def dram_scr(name, shape, dt):
        kind = "ExternalOutput" if name in debug else ("ExternalInput" if name in ext_in else "Internal")
        return nc.dram_tensor(name, list(shape), dt, kind=kind).ap()

    xw = dram_in("xw", [WIN, D]) if (1 in phases or 4 in phases) else None
    w_in = dram_in("w_in", [D, DIN]) if 1 in phases else None
    w_ba = dram_in("w_ba", [2048, D]) if 3 in phases else None
    w_bb = dram_in("w_bb", [2048, D]) if 3 in phases else None
    w_out = dram_in("w_out", [D, D]) if 4 in phases else None
    w_g = dram_in("w_g", [D, DFF]) if 6 in phases else None
    w_u = dram_in("w_u", [D, DFF]) if 6 in phases else None
    w_d = dram_in("w_d", [DFF, D]) if 6 in phases else None
    g_mix = dram_in("g_mix", [1, D])
    g_ffn = dram_in("g_ffn", [1, D])
    g_fin = dram_in("g_fin", [1, D])
    b_gate = dram_in("b_gate", [128, 64])
    b_f = dram_in("b_f", [16, 1])
    identm = dram_in("identm", [128, 128])
    trim = dram_in("trim", [128, 4 * 512])
    beta = dram_in("beta", [1, WIN])
    biasA = dram_in("biasA", [NH * 128, 8 * 512])
    maskA = dram_in("maskA", [128, 2 * 8 * 512])
    out = nc.dram_tensor("out", [T, D], F32, kind="ExternalOutput").ap()

    kbT_d = dram_scr("kbT_d", [NH * 128, WIN], BF16)
    vb_d = dram_scr("vb_d", [WIN, 2048], BF16)
    kaT_d = dram_scr("kaT_d", [NH * 128, 1536], BF16)
    va_d = dram_scr("va_d", [1536, 2048], BF16)
    qT_d = dram_scr("qT_d", [32 * 128, T], BF16)
    sp_d = dram_scr("sp_d", [16, WIN], F32)
    c_d = dram_scr("c_d", [3 * 16, WIN], BF16)
    gates_d = dram_scr("gates_d", [64 * 128, T], BF16)
    oT_d = dram_scr("oT_d", [32 * 128, T], BF16)
    mT_d = dram_scr("mT_d", [32 * 128, T], BF16)
    x1_d = dram_scr("x1_d", [T, D], F32)
    x2_d = dram_scr("x2_d", [T, D], F32)
    hid_d = dram_scr("hid_d", [86 * 128, T], BF16)

    st = contextlib.ExitStack()
    with st:
        def sb(name, shape, dt):
            return st.enter_context(nc.sbuf_tensor(name, list(shape), dt))

        RA = sb("RA", [128, 32768], BF16)
        RW = sb("RW", [128, 32768], BF16)
        RS = sb("RS", [128, 36864], BF16)
        ident = sb("ident", [128, 128], BF16)
        ones_bf = sb("ones_bf", [128, 128], BF16)
        bgate = sb("bgate", [128, 64], F32)
        nbf = sb("nbf", [16, 1], F32)
        stats = sb("stats", [128, 64], F32)
        rstd = sb("rstd", [128, 8], F32)
        epst = sb("epst", [128, 1], F32)
        tri = sb("tri", [128, 4, 512], BF16)
        wf = sb("wf", [128, 32, 16], BF16)
        PB = [st.enter_context(nc.psum_tensor("pb%d" % i, [128, 512], F32)) for i in range(8)]

        S = Sched(nc)

        def ring(i):
            return RW[:, i * 4096:(i + 1) * 4096].rearrange("p (a b) -> p a b", a=8)

        def rs(off, n):
            return RS[:, off:off + n]

        def rsf(off, n):
            return RS[:, off:off + 2 * n].bitcast(F32)

        S.dma("gpsimd", ident[:], identm[:, :], writes=["ident"])
        S.dma("gpsimd", tri[:], trim.rearrange("p (a b) -> p a b", a=4), writes=["tri"])
        S.dma("sync", bgate[:], b_gate[:, :], writes=["bgate"])
        S.dma("sync", nbf[:], b_f[:, :], writes=["nbf"])
        S.add("vector", lambda e: e.tensor_scalar(out=nbf[:], in0=nbf[:], scalar1=-1.0, scalar2=None, op0=ALU.mult),
              reads=["nbf"], writes=["nbf"])
        S.add("vector", lambda e: e.memset(ones_bf[:], 1.0), writes=["ones"])
        S.add("vector", lambda e: e.memset(epst[:], EPS), writes=["epst"])
        if 1 in phases:
            S.dma("gpsimd", wf[:], w_in[:, FC:FC + 16].rearrange("(kt p) c -> p kt c", p=128), writes=["wf"])

        class WStream:
            def __init__(self):
                self.n = 0

            def load(self, src_rows_ap):
                i = self.n % 8
                self.n += 1
                key = "ring%d" % i
                ncols = src_rows_ap.shape[1]
                v = ring(i)
                S.dma("gpsimd", v[:, :, 0:ncols], src_rows_ap.rearrange("(kt p) c -> p kt c", p=128), writes=[key])
                return v, key

        WS = WStream()
        pbn = [0]

        def next_banks(n):
            b = [(pbn[0] + i) % 8 for i in range(n)]
            pbn[0] = (pbn[0] + n) % 8
            return b

        STGKEYS = ["stg0", "stg1", "stg2", "spv"]

        class Norm:
            def __init__(self, src_rows, g_ap, dstT, ntb, key_prefix, ss_from=None, load_g=True):
                self.src_rows, self.g_ap, self.dstT, self.ntb, self.kp, self.ss_from, self.load_g = \
                    src_rows, g_ap, dstT, ntb, key_prefix, ss_from, load_g
                self.Xb = [rsf(0, 4096), rsf(8192, 4096)]
                self.G = rsf(16384, 4096)
                self.HBb = [rs(24576, 4096), rs(28672, 4096)]

            def stats_of(self, tb):
                X = self.Xb[tb % 2]
                xk = "X%d" % (tb % 2)
                S.dma("sync", X, self.src_rows(tb), writes=[xk])
                ssv = stats[:, tb:tb + 1]
                ssk = "ss%d" % tb
                rv = rstd[:, tb:tb + 1]
                rk = "rstd%d" % tb
                if self.ss_from is None:
                    junk = self.HBb[tb % 2]
                    S.add("scalar", lambda e, ssv=ssv, X=X, junk=junk: e.activation(out=junk, in_=X, func=AF.Square, accum_out=ssv),
                          reads=[xk, "ssall"], writes=["HB%d" % (tb % 2), ssk])
                else:
                    src = self.ss_from(tb)
                    S.add("vector", lambda e, ssv=ssv, src=src: e.reduce_sum(out=ssv, in_=src, axis=AX.X),
                          reads=["sspart", "ssall"], writes=[ssk])
                S.add("scalar", lambda e, rv=rv, ssv=ssv: e.activation(out=rv, in_=ssv, func=AF.Sqrt, bias=epst[:, 0:1], scale=1.0 / D),
                      reads=[ssk, "epst"], writes=[rk])

            def start(self):
                if self.load_g:
                    S.dma("sync", self.G, self.g_ap[0:1, :].broadcast_to([128, D]), writes=["G"])
                ntb = self.ntb
                S.add("vector", lambda e: e.memset(stats[:, 0:ntb], 0.0), writes=["ssall", "statsbuf"])
                self.stats_of(0)
                if ntb > 1:
                    self.stats_of(1)

            def tile(self, tb):
                X = self.Xb[tb % 2]
                xk = "X%d" % (tb % 2)
                rv = rstd[:, tb:tb + 1]
                rk = "rstd%d" % tb
                HB = self.HBb[tb % 2]
                hbk = "HB%d" % (tb % 2)
                G = self.G
                S.add("vector", lambda e, rv=rv: e.reciprocal(out=rv, in_=rv), reads=[rk], writes=[rk])
                S.add("vector", lambda e, rv=rv, X=X, HB=HB: e.scalar_tensor_tensor(out=HB, in0=X, scalar=rv, in1=G,
                                                                                   op0=ALU.mult, op1=ALU.mult),
                      reads=[xk, rk, "G"], writes=[hbk])
                for j in range(4):
                    bk = next_banks(1)[0]
                    pv = PB[bk][:].bitcast(BF16).rearrange("p (a b) -> p a b", a=8)

                    def tr(e, j=j, pv=pv, HB=HB):
                        for i in range(8):
                            kt = j * 8 + i
                            ins = e.transpose(out=pv[:, i, :], in_=HB[:, kt * 128:(kt + 1) * 128], identity=ident[:])
                        return ins
                    S.add("tensor", tr, reads=[hbk, "ident"], writes=["pb%d" % bk])
                    dst = self.dstT[:, j * 8:(j + 1) * 8, tb * 128:(tb + 1) * 128]
                    if j == 0:
                        S.add("vector", lambda e, dst=dst, pv=pv: e.tensor_copy(out=dst, in_=pv),
                              reads=["pb%d" % bk], writes=["%s%d" % (self.kp, tb)])
                    else:
                        S.add("scalar", lambda e, dst=dst, pv=pv: e.copy(out=dst, in_=pv),
                              reads=["pb%d" % bk], writes=["%s%d" % (self.kp, tb)])
                if tb + 2 < self.ntb:
                    self.stats_of(tb + 2)

        def rmsnorm_to_T(src_rows, g_ap, dstT, ntb, key_prefix, ss_from=None):
            n_ = Norm(src_rows, g_ap, dstT, ntb, key_prefix, ss_from)
            n_.start()
            for tb in range(ntb):
                n_.tile(tb)

        def run_jobs(jobs):
            def do_load(job):
                ng = job["nk"] // 8
                return [WS.load(job["w"][job["row0"] + g * 1024: job["row0"] + (g + 1) * 1024,
                                         job["c0"]:job["c0"] + job["ncols"]]) for g in range(ng)]
            real = [i_ for i_, j_ in enumerate(jobs) if j_["kind"] != "call"]
            nxt_real = {real[i_]: (real[i_ + 1] if i_ + 1 < len(real) else None) for i_ in range(len(real))}
            nxt = do_load(jobs[real[0]])
            for n, job in enumerate(jobs):
                if job["kind"] == "call":
                    job["fn"]()
                    continue
                tiles = nxt
                if nxt_real[n] is not None:
                    nxt = do_load(jobs[nxt_real[n]])
                nk = job["nk"]
                actT = job["act"]
                rkeys = [t[1] for t in tiles] + list(job.get("keys", ()))
                if job["kind"] == "T1":
                    tok0 = job["tok0"]
                    nhalf = job["ntok"] // 512
                    for cbi in range(job["ncols"] // 128):
                        banks = next_banks(nhalf)

                        def mm(e, tiles=tiles, cbi=cbi, banks=banks, nk=nk, actT=actT, tok0=tok0, nhalf=nhalf):
                            for kt in range(nk):
                                tv = tiles[kt // 8][0]
                                for hf in range(nhalf):
                                    ins = e.matmul(PB[banks[hf]][:, :], lhsT=tv[:, kt % 8, cbi * 128:(cbi + 1) * 128],
                                                   rhs=actT[:, kt, tok0 + hf * 512: tok0 + (hf + 1) * 512],
                                                   start=(kt == 0), stop=(kt == nk - 1))
                            return ins
                        S.add("tensor", mm, reads=rkeys, writes=["pb%d" % b_ for b_ in banks])
                        job["ep"](cbi, banks)
                else:
                    for tb in job["tbs"]:
                        bk = next_banks(1)[0]

                        def mm(e, tiles=tiles, tb=tb, bk=bk, nk=nk, actT=actT):
                            for kt in range(nk):
                                tv = tiles[kt // 8][0]
                                ins = e.matmul(PB[bk][:, :], lhsT=actT[:, kt, tb * 128:(tb + 1) * 128], rhs=tv[:, kt % 8, :],
                                               start=(kt == 0), stop=(kt == nk - 1))
                            return ins
                        S.add("tensor", mm, reads=rkeys, writes=["pb%d" % bk])
                        job["ep"](tb, bk)

        def jobs_T1(w, c0, ncols, act, keys, tok0, ntok, epf, nk=32, row0=0):
            out_ = []
            for cg in range((ncols + 511) // 512):
                n_ = min(512, ncols - cg * 512)
                out_.append(dict(kind="T1", w=w, row0=row0, nk=nk, c0=c0 + cg * 512, ncols=n_, act=act, keys=keys,
                                 tok0=tok0, ntok=ntok, ep=(lambda cbi, banks, cg=cg: epf(cg * 4 + cbi, banks))))
            return out_

        def jobs_T2(w, c0, ncols, act, keys, tbs, epf, nk=32, row0=0):
            return [dict(kind="T2", w=w, row0=row0, nk=nk, c0=c0 + cg * 512, ncols=512, act=act, keys=keys, tbs=tbs,
                         ep=(lambda tb, bk, cg=cg: epf(cg, tb, bk))) for cg in range(ncols // 512)]

        stg_n = [0]

        def stg_bf(ncols):
            i = stg_n[0] % 3
            stg_n[0] += 1
            return RS[:, 32768 + i * 1024: 32768 + i * 1024 + ncols], "stg%d" % i

        ev_n = [0]

        def evac_copy(dst, src, reads, writes, scale=None):
            ev_n[0] += 1
            if ev_n[0] % 2 == 0:
                if scale is None:
                    S.add("vector", lambda e: e.tensor_copy(out=dst, in_=src), reads=reads, writes=writes)
                else:
                    S.add("vector", lambda e: e.tensor_scalar(out=dst, in0=src, scalar1=scale, scalar2=None, op0=ALU.mult),
                          reads=reads, writes=writes)
            else:
                if scale is None:
                    S.add("scalar", lambda e: e.copy(out=dst, in_=src), reads=reads, writes=writes)
                else:
                    S.add("scalar", lambda e: e.mul(out=dst, in_=src, mul=scale), reads=reads, writes=writes)

        hT = RA[:, :].rearrange("p (a b) -> p a b", a=32)
        SCALE = float(HD) ** -0.5

        def ep_featT(dst_d, head0, tokd0, ntok, tok_sb0=0, scale=None):
            def ep(cb, banks):
                sv, sk = stg_bf(ntok)
                for hf, bk in enumerate(banks):
                    evac_copy(sv[:, hf * 512:(hf + 1) * 512], PB[bk][:, :], ["pb%d" % bk], [sk], scale=scale)
                r0 = (head0 + cb) * 128
                S.dma("sync", dst_d[r0:r0 + 128, tokd0:tokd0 + ntok], sv, reads=[sk])
            return ep

        def ep_tokmajor(dst_d, rowd0, cold0):
            def ep(cg, tb, bk):
                sv, sk = stg_bf(512)
                evac_copy(sv, PB[bk][:, :], ["pb%d" % bk], [sk])
                S.dma("sync", dst_d[rowd0 + tb * 128: rowd0 + (tb + 1) * 128, cold0 + cg * 512: cold0 + (cg + 1) * 512], sv,
                      reads=[sk])
            return ep

        if 1 in phases:
            hkeys = ["hT%d" % tb for tb in range(8)]
            norms = [Norm((lambda tb, ch=ch: xw[ch * 1024 + tb * 128: ch * 1024 + (tb + 1) * 128, :]), g_mix, hT, 8, "hT",
                          load_g=(ch == 0)) for ch in range(4)]
            norms[0].start()
            for tb in range(8):
                norms[0].tile(tb)

            def call(fn):
                return dict(kind="call", fn=fn)

            def interleave(jobs_, fns):
                out_ = []
                for i_, j_ in enumerate(jobs_):
                    out_.append(j_)
                    if i_ < len(fns):
                        out_.append(call(fns[i_]))
                return out_

            def ep_gate(cb, banks):
                sv, sk = stg_bf(1024)
                for hf, bk in enumerate(banks):
                    S.add("scalar", lambda e, hf=hf, bk=bk, cb=cb, sv=sv: e.activation(
                        out=sv[:, hf * 512:(hf + 1) * 512], in_=PB[bk][:, :], func=AF.Sigmoid, bias=bgate[:, cb:cb + 1]),
                        reads=["pb%d" % bk, "bgate"], writes=[sk])
                S.dma("sync", gates_d[cb * 128:(cb + 1) * 128, :], sv, reads=[sk])

            def flogits(ch):
                banks = next_banks(2)

                def mmf(e, banks=banks):
                    for kt in range(32):
                        for hf in range(2):
                            ins = e.matmul(PB[banks[hf]][0:16, :], lhsT=wf[:, kt, :], rhs=hT[:, kt, hf * 512:(hf + 1) * 512],
                                           start=(kt == 0), stop=(kt == 31))
                    return ins
                S.add("tensor", mmf, reads=["wf"] + hkeys, writes=["pb%d" % b_ for b_ in banks])
                spv = rsf(35840, 512)
                for hf, bk in enumerate(banks):
                    S.add("scalar", lambda e, bk=bk: e.activation(out=spv[0:16, :], in_=PB[bk][0:16, :],
                                                                 func=AF.Exp, bias=nbf[:, 0:1], scale=-1.0),
                          reads=["pb%d" % bk, "nbf"], writes=["spv"])
                    S.add("scalar", lambda e: e.activation(out=spv[0:16, :], in_=spv[0:16, :], func=AF.Ln, bias=1.0, scale=1.0),
                          reads=["spv"], writes=["spv"])
                    S.dma("sync", sp_d[:, ch * 1024 + hf * 512: ch * 1024 + (hf + 1) * 512], spv[0:16, :], reads=["spv"])

            for ch in range(4):
                own = (ch == 3)
                k03 = ["hT%d" % tb for tb in range(4)]
                k47 = ["hT%d" % tb for tb in range(4, 8)]
                head = jobs_T2(w_in, VB, 2048, hT, k03, [0, 1, 2, 3], ep_tokmajor(vb_d, ch * 1024, 0))
                if ch > 0:
                    head = interleave(head, [(lambda tb=tb, ch=ch: norms[ch].tile(tb)) for tb in range(4, 8)])
                mid = jobs_T1(w_in, KB, 2048, hT, hkeys, 0, 1024, ep_featT(kbT_d, 0, ch * 1024, 1024))
                if ch == 2:
                    mid += jobs_T1(w_in, KA, 2048, hT, k47, 512, 512, ep_featT(kaT_d, 0, 0, 512))
                    mid += jobs_T2(w_in, VA, 2048, hT, k47, [4, 5, 6, 7], ep_tokmajor(va_d, -512, 0))
                if own:
                    mid += jobs_T1(w_in, KA, 2048, hT, hkeys, 0, 1024, ep_featT(kaT_d, 0, 512, 1024))
                    mid += jobs_T2(w_in, VA, 2048, hT, hkeys, list(range(8)), ep_tokmajor(va_d, 512, 0))
                    mid += jobs_T1(w_in, QA, 2048, hT, hkeys, 0, 1024, ep_featT(qT_d, 0, 0, 1024, scale=SCALE))
                    mid += jobs_T1(w_in, QB, 2048, hT, hkeys, 0, 1024, ep_featT(qT_d, 16, 0, 1024, scale=SCALE))
                    mid += jobs_T1(w_in, GA, 8192, hT, hkeys, 0, 1024, ep_gate)
                mid.append(call(lambda ch=ch: flogits(ch)))
                tail = jobs_T2(w_in, VB, 2048, hT, k47, [4, 5, 6, 7], ep_tokmajor(vb_d, ch * 1024, 0))
                if ch < 3:
                    mid.insert(max(0, len(mid) - 3), call(lambda ch=ch: norms[ch + 1].start()))
                    tail = interleave(tail, [(lambda tb=tb, ch=ch: norms[ch + 1].tile(tb)) for tb in range(4)])
                run_jobs(head + mid + tail)
        S.barrier()

        cum_ops = []
        if 15 in phases:
            A0 = RA[:, 16384:24576].bitcast(F32)
            A1 = RA[:, 24576:32768].bitcast(F32)
            cum_ops.append(lambda: S.dma("sync", A0[0:16, :], sp_d[:, :], writes=["A0"]))
            cur, oth, ck, ok = A0, A1, "A0", "A1"
            sft = 1
            while sft < WIN:
                def stp(e, cur=cur, oth=oth, sft=sft):
                    e.tensor_copy(out=oth[0:16, 0:sft], in_=cur[0:16, 0:sft])
                    return e.tensor_tensor(out=oth[0:16, sft:WIN], in0=cur[0:16, sft:WIN], in1=cur[0:16, 0:WIN - sft], op=ALU.add)
                cum_ops.append(lambda stp=stp, ck=ck, ok=ok: S.add("vector", stp, reads=[ck], writes=[ok]))
                cur, oth, ck, ok = oth, cur, ok, ck
                sft *= 2
            Sc, Sk = cur, ck
            Hb = RA[:, 24576:28672] if Sk == "A0" else RA[:, 16384:20480]
            Hk = ok
            for lvl in range(3):
                cum_ops.append(lambda: S.add("vector", lambda e: e.tensor_copy(out=Hb[0:16, :], in_=Sc[0:16, :]), reads=[Sk], writes=[Hk]))
                cum_ops.append(lambda lvl=lvl: S.dma("sync", c_d[lvl * 16:(lvl + 1) * 16, :], Hb[0:16, :], reads=[Hk], writes=["c_d"]))
                if lvl < 2:
                    cum_ops.append(lambda: S.add("vector", lambda e: e.tensor_tensor(out=Sc[0:16, :], in0=Sc[0:16, :], in1=Hb[0:16, :], op=ALU.subtract),
                                                 reads=[Sk, Hk], writes=[Sk]))
            if 2 not in phases:
                for f in cum_ops:
                    f()
                cum_ops = []
                S.barrier()

        if 2 in phases:
            oT = RA[:, :].rearrange("p (a b) -> p a b", a=32)
            KT = [RW[:, i * 4096:(i + 1) * 4096] for i in range(2)]
            VV = [RW[:, 8192 + i * 4096: 8192 + (i + 1) * 4096].rearrange("p (a b) -> p a b", b=128) for i in range(2)]
            AL = [RW[:, 16384 + i * 4096: 16384 + (i + 1) * 4096] for i in range(2)]
            QT = [RW[:, 24576 + i * 1024: 24576 + (i + 1) * 1024] for i in range(2)]
            AR = [RW[:, 26624 + i * 1024: 26624 + (i + 1) * 1024] for i in range(2)]
            PT = [RW[:, 28672 + i * 512: 28672 + (i + 1) * 512] for i in range(4)]
            BM = [[RS[:, (2 * i + q) * 4096:(2 * i + q + 1) * 4096].rearrange("p (a b) -> p a b", a=8) for q in range(2)] for i in range(2)]
            BST = [RS[:, 16384 + i * 4096: 16384 + (i + 1) * 4096] for i in range(2)]
            MA = RS[:, 24576:32768].rearrange("p (q a b) -> p q a b", q=2, a=8)
            RC = [rsf(32768 + i * 1024, 512) for i in range(2)]
            for i in range(2):
                S.add("vector", lambda e, i=i: e.memset(AL[i][0:8, :], 0.0), writes=["AL%d" % i])
                S.add("vector", lambda e, i=i: e.memset(AL[i][0:3, :], -1.0), reads=["AL%d" % i], writes=["AL%d" % i])
                S.dma("gpsimd", AL[i][6:7, :], beta[0:1, :], reads=["AL%d" % i], writes=["AL%d" % i])
                S.add("vector", lambda e, i=i: e.memset(AR[i][0:8, :], 1.0), writes=["AR%d" % i])
            S.dma("gpsimd", MA, maskA.rearrange("p (q a b) -> p q a b", q=2, a=8), writes=["MA"])
            S.barrier()

            import os
            KD = os.environ.get('KDBG', 'AB')

            def attention_head(hslot, nkeys_fn, loads, kt_out, mask_mm):
                i = hslot % 2
                keys = loads(i)
                for qt in range(2):
                    kbs = nkeys_fn(qt)
                    ob = 4 + (qt % 2)
                    lb = 6 + (qt % 2)
                    n = len(kbs)
                    pending = []

                    def emit_pv(idx, kb, pt_i, first, last, ob=ob, lb=lb):
                        def pv(e):
                            e.matmul(PB[ob][:, :], lhsT=VV[i][:, kb, :], rhs=PT[pt_i], start=first, stop=last)
                            return e.matmul(PB[lb][:, :], lhsT=ones_bf[:, :], rhs=PT[pt_i], start=first, stop=last)
                        S.add("tensor", pv, reads=["PT%d" % pt_i] + keys + ["ones"], writes=["pb%d" % ob, "pb%d" % lb])

                    for idx, kb in enumerate(kbs):
                        sbk = idx % 4

                        def qk(e, kb=kb, sbk=sbk, idx=idx, qt=qt):
                            ins = e.matmul(PB[sbk][:, :], lhsT=KT[i][:, kb * 128:(kb + 1) * 128], rhs=QT[i][:, qt * 512:(qt + 1) * 512],
                                           start=True, stop=False)
                            ins = mask_mm(e, i, qt, idx, kb, sbk)
                            return ins
                        S.add("tensor", qk, reads=keys + ["ident", "tri"], writes=["pb%d" % sbk])
                        pt_i = idx % 4
                        S.add("scalar", lambda e, sbk=sbk, pt_i=pt_i: e.activation(out=PT[pt_i], in_=PB[sbk][:, :], func=AF.Exp),
                              reads=["pb%d" % sbk], writes=["PT%d" % pt_i])
                        pending.append((idx, kb, pt_i))
                        if len(pending) > 2:
                            a = pending.pop(0)
                            emit_pv(a[0], a[1], a[2], a[0] == 0, a[0] == n - 1)
                    while pending:
                        a = pending.pop(0)
                        emit_pv(a[0], a[1], a[2], a[0] == 0, a[0] == n - 1)
                    rc = RC[qt % 2]
                    S.add("vector", lambda e, rc=rc, lb=lb: e.reciprocal(out=rc, in_=PB[lb][:, :]), reads=["pb%d" % lb], writes=["RC%d" % (qt % 2)])
                    dst = oT[:, kt_out, qt * 512:(qt + 1) * 512]
                    S.add("vector", lambda e, rc=rc, ob=ob, dst=dst: e.tensor_tensor(out=dst, in0=PB[ob][:, :], in1=rc, op=ALU.mult),
                          reads=["pb%d" % ob, "RC%d" % (qt % 2)], writes=["oT%d_%d" % (kt_out, qt)])

            def prepA(h):
                i = h % 2
                S.dma("gpsimd", BST[i], biasA[h * 128:(h + 1) * 128, :], writes=["BST%d" % i])
                for qt in range(2):
                    S.add("vector", lambda e, qt=qt, i=i: e.tensor_tensor(out=BM[i][qt][:, :, :].rearrange("p a b -> p (a b)"), in0=BST[i],
                                                                          in1=MA[:, qt, :, :].rearrange("p a b -> p (a b)"), op=ALU.add),
                          reads=["BST%d" % i, "MA"], writes=["BM%d_%d" % (i, qt)])

            if 'A' in KD:
                prepA(0)
            for h in (range(NH) if 'A' in KD else []):
                def loadsA(i, h=h):
                    S.dma("sync", KT[i][:, 0:1536], kaT_d[h * 128:(h + 1) * 128, :], writes=["KT%d" % i])
                    S.dma("sync", VV[i][:, 0:12, :], va_d[:, h * 128:(h + 1) * 128].rearrange("(kb p) d -> p kb d", p=128), writes=["VV%d" % i])
                    S.dma("sync", QT[i], qT_d[h * 128:(h + 1) * 128, :], writes=["QT%d" % i])
                    return ["KT%d" % i, "VV%d" % i, "QT%d" % i, "BM%d_0" % i, "BM%d_1" % i]

                if h + 1 < NH:
                    prepA(h + 1)

                def maskA_mm(e, i, qt, idx, kb, sbk):
                    return e.matmul(PB[sbk][:, :], lhsT=ident[:, :], rhs=BM[i][qt][:, idx, :], start=False, stop=True)
                attention_head(h, lambda qt: [4 * qt + j for j in range(8)], loadsA, h, maskA_mm)
                for _ in range(2):
                    if cum_ops:
                        cum_ops.pop(0)()
            while cum_ops:
                cum_ops.pop(0)()

            for h in (range(NH) if 'B' in KD else []):
                def loadsB(i, h=h):
                    S.dma("sync", KT[i][:, :], kbT_d[h * 128:(h + 1) * 128, :], writes=["KT%d" % i])
                    S.dma("sync", VV[i][:, :, :], vb_d[:, h * 128:(h + 1) * 128].rearrange("(kb p) d -> p kb d", p=128), writes=["VV%d" % i])
                    S.dma("sync", QT[i], qT_d[(16 + h) * 128:(17 + h) * 128, :], writes=["QT%d" % i])
                    S.dma("sync", AL[i][3:6, :], c_d[:, :].rearrange("(l h) w -> l h w", h=16)[0:3, h, :], reads=["c_d"], writes=["AL%d" % i])
                    S.dma("sync", AR[i][0:3, :], c_d[:, :].rearrange("(l h) w -> l h w", h=16)[0:3, h, 3072:4096], reads=["c_d"], writes=["AR%d" % i])
                    return ["KT%d" % i, "VV%d" % i, "QT%d" % i, "AL%d" % i, "AR%d" % i]

                def maskB_mm(e, i, qt, idx, kb, sbk):
                    j = kb - 24 - 4 * qt
                    ins = e.matmul(PB[sbk][:, :], lhsT=AL[i][0:8, kb * 128:(kb + 1) * 128], rhs=AR[i][0:8, qt * 512:(qt + 1) * 512],
                                   start=False, stop=(j < 0))
                    if j >= 0:
                        ins = e.matmul(PB[sbk][:, :], lhsT=ident[:, :], rhs=tri[:, j, :], start=False, stop=True)
                    return ins
                attention_head(h, lambda qt: list(range(24 + 4 * (qt + 1))), loadsB, 16 + h, maskB_mm)
            if "oT_d" in debug:
                S.dma("sync", oT_d.rearrange("(kt p) t -> p kt t", p=128), oT, reads=["oT%d_%d" % (k, q) for k in range(32) for q in range(2)],
                      )
            S.barrier()

        if 3 in phases:
            GT = [RS[:, i * 1024:(i + 1) * 1024] for i in range(4)]
            T2_ = [rsf(4096 + i * 1024, 512) for i in range(2)]
            TM = [rsf(16384 + i * 2048, 1024) for i in range(4)]
            gt_n = [0]
            oTv = RA[:, :].rearrange("p (a b) -> p a b", a=32)

            def ep_ua(cb, banks):
                gi = gt_n[0] % 4
                gt_n[0] += 1
                S.dma("sync", GT[gi], gates_d[cb * 128:(cb + 1) * 128, :], writes=["GT%d" % gi])
                tm = TM[cb % 4]
                for hf in range(2):
                    S.add("vector", lambda e, tm=tm, hf=hf, gi=gi, ba=banks[hf]: e.tensor_tensor(
                        out=tm[:, hf * 512:(hf + 1) * 512], in0=PB[ba][:, :], in1=GT[gi][:, hf * 512:(hf + 1) * 512], op=ALU.mult),
                        reads=["pb%d" % banks[hf], "GT%d" % gi], writes=["TM%d_%d" % (cb % 4, hf)])

            def ep_ub(cb, banks):
                gi = gt_n[0] % 4
                gt_n[0] += 1
                S.dma("sync", GT[gi], gates_d[(32 + cb) * 128:(33 + cb) * 128, :], writes=["GT%d" % gi])
                tm = TM[cb % 4]
                sv, sk = stg_bf(1024)
                for hf in range(2):
                    t2 = T2_[hf]
                    S.add("vector", lambda e, t2=t2, hf=hf, gi=gi, bb=banks[hf]: e.tensor_tensor(
                        out=t2, in0=PB[bb][:, :], in1=GT[gi][:, hf * 512:(hf + 1) * 512], op=ALU.mult),
                        reads=["pb%d" % banks[hf], "GT%d" % gi], writes=["T2_%d" % hf])
                    S.add("gpsimd", lambda e, t2=t2, tm=tm, sv=sv, hf=hf: e.tensor_tensor(
                        out=sv[:, hf * 512:(hf + 1) * 512], in0=t2, in1=tm[:, hf * 512:(hf + 1) * 512], op=ALU.add),
                        reads=["T2_%d" % hf, "TM%d_%d" % (cb % 4, hf)], writes=[sk])
                S.dma("sync", mT_d[cb * 128:(cb + 1) * 128, :], sv, reads=[sk])

            jobs = []
            for cg in range(8):
                jobs.append(dict(kind="T1", w=w_ba, row0=0, nk=16, c0=cg * 512, ncols=512, act=oTv[:, 0:16, :], keys=[],
                                 tok0=0, ntok=1024, ep=(lambda cbi, banks, cg=cg: ep_ua(cg * 4 + cbi, banks))))
                jobs.append(dict(kind="T1", w=w_bb, row0=0, nk=16, c0=cg * 512, ncols=512, act=oTv[:, 16:32, :], keys=[],
                                 tok0=0, ntok=1024, ep=(lambda cbi, banks, cg=cg: ep_ub(cg * 4 + cbi, banks))))
            run_jobs(jobs)
            S.barrier()

        if 4 in phases:
            mT = RA[:, :].rearrange("p (a b) -> p a b", a=32)
            S.dma("sync", mT, mT_d.rearrange("(kt p) t -> p kt t", p=128), writes=["mT"])
            XT = [rsf(i * 1024, 512) for i in range(4)]
            OT = [rsf(4096 + i * 1024, 512) for i in range(4)]
            SQ4 = rs(28672, 512)
            n4 = [0]

            def make_ep_resid(src_rows_ap, dst_d):
                def ep(cg, tb, bk):
                    i = n4[0] % 4
                    n4[0] += 1
                    S.dma("sync", XT[i], src_rows_ap[tb * 128:(tb + 1) * 128, cg * 512:(cg + 1) * 512], writes=["XT%d" % i])
                    S.add("vector", lambda e, i=i, bk=bk: e.tensor_tensor(out=OT[i], in0=PB[bk][:, :], in1=XT[i], op=ALU.add),
                          reads=["pb%d" % bk, "XT%d" % i], writes=["OT%d" % i])
                    ssv = stats[:, tb * 8 + cg: tb * 8 + cg + 1]
                    S.add("scalar", lambda e, i=i, ssv=ssv: e.activation(out=SQ4, in_=OT[i], func=AF.Square, accum_out=ssv),
                          reads=["OT%d" % i, "stats0"], writes=["SQ4", "sspart"])
                    S.dma("sync", dst_d[tb * 128:(tb + 1) * 128, cg * 512:(cg + 1) * 512], OT[i], reads=["OT%d" % i])
                return ep
            S.add("vector", lambda e: e.memset(stats[:, :], 0.0), writes=["stats0"])
            run_jobs(jobs_T2(w_out, 0, D, mT, ["mT"], list(range(8)), make_ep_resid(xw[3072:4096, :], x1_d)))
            S.barrier()

        if 5 in phases:
            h2T = RA[:, :].rearrange("p (a b) -> p a b", a=32)
            sstmp = sb("sstmp", [128, 64], F32)
            S.add("vector", lambda e: e.tensor_copy(out=sstmp[:, :], in_=stats[:, :]), reads=["statsbuf"], writes=["sspart"])
            rmsnorm_to_T(lambda tb: x1_d[tb * 128:(tb + 1) * 128, :], g_ffn, h2T, 8, "h2T",
                         ss_from=(lambda tb: sstmp[:, tb * 8:(tb + 1) * 8]) if 4 in phases else None)
            S.barrier()

        if 6 in phases:
            SL = [rsf(16384 + i * 2048, 1024) for i in range(4)]

            def ep_g(fb, banks):
                sl = SL[fb % 4]
                for hf in range(2):
                    S.add("scalar", lambda e, sl=sl, hf=hf, b=banks[hf]: e.activation(out=sl[:, hf * 512:(hf + 1) * 512], in_=PB[b][:, :], func=AF.Silu),
                          reads=["pb%d" % banks[hf]], writes=["SL%d_%d" % (fb % 4, hf)])

            def ep_u(fb, banks):
                sl = SL[fb % 4]
                sv, sk = stg_bf(1024)
                for hf in range(2):
                    S.add("vector", lambda e, sl=sl, b=banks[hf], sv=sv, hf=hf: e.tensor_tensor(
                        out=sv[:, hf * 512:(hf + 1) * 512], in0=PB[b][:, :], in1=sl[:, hf * 512:(hf + 1) * 512], op=ALU.mult),
                        reads=["pb%d" % banks[hf], "SL%d_%d" % (fb % 4, hf)], writes=[sk])
                S.dma("sync", hid_d[fb * 128:(fb + 1) * 128, :], sv, reads=[sk])

            jobs = []
            for fg in range(22):
                n_ = min(512, DFF - fg * 512)
                jobs.append(dict(kind="T1", w=w_g, row0=0, nk=32, c0=fg * 512, ncols=n_, act=h2T, keys=[], tok0=0, ntok=1024,
                                 ep=(lambda cbi, banks, fg=fg: ep_g(fg * 4 + cbi, banks))))
                jobs.append(dict(kind="T1", w=w_u, row0=0, nk=32, c0=fg * 512, ncols=n_, act=h2T, keys=[], tok0=0, ntok=1024,
                                 ep=(lambda cbi, banks, fg=fg: ep_u(fg * 4 + cbi, banks))))
            run_jobs(jobs)
            S.barrier()

            NRES_A, NRES_S = 32, 13
            NRES = NRES_A + NRES_S
            HRA = RA[:, :].rearrange("p (a b) -> p a b", a=32)
            HRS = RS[:, 10240:10240 + NRES_S * 1024].rearrange("p (a b) -> p a b", a=NRES_S)
            HC = [RW[:, 16384 + i * 8192: 16384 + (i + 1) * 8192].rearrange("p (a b) -> p a b", a=8) for i in range(2)]
            XT = [rsf(i * 1024, 512) for i in range(8)]
            SQ4 = rs(8192, 512)
            hc_n = [0]
            wn = [0]
            S.add("vector", lambda e: e.memset(stats[:, :], 0.0), writes=["stats0"])
            for g in range(4):
                S.dma("sync", HRA[:, g * 8:(g + 1) * 8, :], hid_d[g * 1024:(g + 1) * 1024, :].rearrange("(kt p) t -> p kt t", p=128),
                      writes=["HRA%d" % g])
            S.dma("sync", HRS, hid_d[NRES_A * 128:NRES * 128, :].rearrange("(kt p) t -> p kt t", p=128), writes=["HRS"])
            chunks = [(g * 8, 8, "A", g) for g in range(4)] + [(32, 8, "S", 0), (40, 5, "S", 8)]
            f0 = NRES
            while f0 < 86:
                nf = min(8, 86 - f0)
                chunks.append((f0, nf, "H", None))
                f0 += nf
            nch = len(chunks)
            for cg in range(8):
                for tb in range(8):
                    S.dma("sync", XT[tb], x1_d[tb * 128:(tb + 1) * 128, cg * 512:(cg + 1) * 512], writes=["XT%d" % tb])
                for ci, (f0, nf, kind, arg) in enumerate(chunks):
                    if kind == "A":
                        hv, hk = HRA[:, arg * 8:(arg + 1) * 8, :], "HRA%d" % arg
                    elif kind == "S":
                        hv, hk = HRS[:, arg:arg + nf, :], "HRS"
                    else:
                        hi_ = hc_n[0] % 2
                        hc_n[0] += 1
                        hv, hk = HC[hi_], "HC%d" % hi_
                        S.dma("sync", hv[:, 0:nf, :], hid_d[f0 * 128:(f0 + nf) * 128, :].rearrange("(kt p) t -> p kt t", p=128),
                              writes=[hk])
                    wi = wn[0] % 4
                    wn[0] += 1
                    wv = ring(wi)
                    S.dma("gpsimd", wv[:, 0:nf, :], w_d[f0 * 128:(f0 + nf) * 128, cg * 512:(cg + 1) * 512].rearrange("(kt p) c -> p kt c", p=128),
                          writes=["ring%d" % wi])

                    def mm_tb(e, tb, hv=hv, wv=wv, nf=nf, ci=ci):
                        for k in range(nf):
                            ins = e.matmul(PB[tb][:, :], lhsT=hv[:, k, tb * 128:(tb + 1) * 128], rhs=wv[:, k, :],
                                           start=(ci == 0 and k == 0), stop=(ci == nch - 1 and k == nf - 1))
                        return ins
                    if ci == 0 or ci == nch - 1:
                        for tb in range(8):
                            S.add("tensor", lambda e, tb=tb, f=mm_tb: f(e, tb), reads=[hk, "ring%d" % wi], writes=["pb%d" % tb])
                    else:
                        def mm(e, f=mm_tb):
                            for tb in range(8):
                                ins = f(e, tb)
                            return ins
                        S.add("tensor", mm, reads=[hk, "ring%d" % wi], writes=["pb%d" % b_ for b_ in range(8)])
                for tb in range(8):
                    S.add("vector", lambda e, tb=tb: e.tensor_tensor(out=XT[tb], in0=PB[tb][:, :], in1=XT[tb], op=ALU.add),
                          reads=["pb%d" % tb, "XT%d" % tb], writes=["XT%d" % tb])
                    ssv = stats[:, tb * 8 + cg: tb * 8 + cg + 1]
                    S.add("scalar", lambda e, tb=tb, ssv=ssv: e.activation(out=SQ4, in_=XT[tb], func=AF.Square, accum_out=ssv),
                          reads=["XT%d" % tb, "stats0"], writes=["SQ4", "sspart"])
                    S.dma("sync", x2_d[tb * 128:(tb + 1) * 128, cg * 512:(cg + 1) * 512], XT[tb], reads=["XT%d" % tb])
            S.barrier()

            Xs = [rsf(0, 4096), rsf(8192, 4096)]
            Os = [rsf(24576, 2048), rsf(28672, 2048)]
            G = rsf(16384, 4096)
            S.dma("sync", G, g_fin[0:1, :].broadcast_to([128, D]), writes=["G"])
            for tb in range(2):
                S.dma("sync", Xs[tb], x2_d[tb * 128:(tb + 1) * 128, :], writes=["X7_%d" % tb])
            for tb in range(8):
                X = Xs[tb % 2]
                xk = "X7_%d" % (tb % 2)
                ssv = rstd[:, tb:tb + 1]
                rk = "r7_%d" % tb
                S.add("vector", lambda e, ssv=ssv, tb=tb: e.reduce_sum(out=ssv, in_=stats[:, tb * 8:(tb + 1) * 8], axis=AX.X),
                      reads=["sspart"], writes=[rk])
                S.add("scalar", lambda e, ssv=ssv: e.activation(out=ssv, in_=ssv, func=AF.Sqrt, bias=epst[:, 0:1], scale=1.0 / D),
                      reads=[rk, "epst"], writes=[rk])
                S.add("vector", lambda e, ssv=ssv: e.reciprocal(out=ssv, in_=ssv), reads=[rk], writes=[rk])
                for hf in range(2):
                    O = Os[hf]
                    eng = "vector"
                    S.add(eng, lambda e, ssv=ssv, X=X, O=O, hf=hf: e.scalar_tensor_tensor(
                        out=O, in0=X[:, hf * 2048:(hf + 1) * 2048], scalar=ssv, in1=G[:, hf * 2048:(hf + 1) * 2048],
                        op0=ALU.mult, op1=ALU.mult),
                        reads=[xk, rk, "G"], writes=["O7_%d" % hf])
                    S.dma("sync", out[tb * 128:(tb + 1) * 128, hf * 2048:(hf + 1) * 2048], O, reads=["O7_%d" % hf])
                if tb + 2 < 8:
                    S.dma("sync", Xs[tb % 2], x2_d[(tb + 2) * 128:(tb + 3) * 128, :], writes=[xk])
            S.barrier()

        S.barrier()
        S.emit()
        build.stats = S.stats
    return nc


def host_inputs(x, g_mix, w_in, b_f, b_gate, rel_bias, w_branch_a, w_branch_b, w_out,
                g_ffn, w_gate_ffn, w_up_ffn, w_down_ffn, g_final):
    x = np.asarray(x, np.float32)
    f32 = lambda a: np.ascontiguousarray(np.asarray(a, np.float32))
    shared = {
        "w_in": f32(w_in[0]), "w_ba": f32(w_branch_a[0]), "w_bb": f32(w_branch_b[0]), "w_out": f32(w_out[0]),
        "w_g": f32(w_gate_ffn[0]), "w_u": f32(w_up_ffn[0]), "w_d": f32(w_down_ffn[0]),
        "g_mix": f32(g_mix[0]).reshape(1, D), "g_ffn": f32(g_ffn[0]).reshape(1, D), "g_fin": f32(g_final).reshape(1, D),
        "b_gate": f32(np.asarray(b_gate[0]).reshape(64, 128).T),
        "b_f": f32(np.asarray(b_f[0]).reshape(16, 1)),
        "identm": np.eye(128, dtype=np.float32),
    }
    k = np.arange(128)[:, None, None]
    j = np.arange(4)[None, :, None]
    q = np.arange(512)[None, None, :]
    shared["trim"] = np.where(j * 128 + k <= q, 0.0, NEG).astype(np.float32).reshape(128, 2048)
    k = np.arange(128)[:, None, None]
    j = np.arange(8)[None, :, None]
    q = np.arange(512)[None, None, :]
    dist = q - j * 128 - k + 512
    ridx = np.clip(dist, -256, 256) + 256
    rb = np.asarray(rel_bias[0], np.float32)
    shared["biasA"] = np.ascontiguousarray(rb[:, ridx]).reshape(NH * 128, 8 * 512)
    maps = []
    for c in range(8):
        b, r = c // 4, c % 4
        end = (r + 1) * 1024
        start = end - WIN
        xwin = np.zeros((WIN, D), np.float32)
        if start < 0:
            xwin[-start:] = x[b, 0:end]
        else:
            xwin[:] = x[b, start:end]
        wpos = start + np.arange(WIN)
        beta = np.where(wpos >= 0, 0.0, NEG).astype(np.float32).reshape(1, WIN)
        k4 = np.arange(128)[:, None, None, None]
        qt4 = np.arange(2)[None, :, None, None]
        j4 = np.arange(8)[None, None, :, None]
        q4 = np.arange(512)[None, None, None, :]
        pq = qt4 * 512 + q4
        pk = qt4 * 512 + j4 * 128 + k4 - 512
        cq = pq // 64
        ck = np.floor_divide(pk, 64)
        ok = (ck >= cq - 8) & (ck <= cq) & ((r * 1024 + pk) >= 0)
        maskA = np.where(ok, 0.0, NEG).astype(np.float32).reshape(128, 2 * 8 * 512)
        m = dict(shared)
        m["xw"] = xwin
        m["beta"] = beta
        m["maskA"] = np.ascontiguousarray(maskA)
        maps.append(m)
    return maps


_NC_CACHE = {}


def kernel(**inputs):
    maps = host_inputs(**inputs)
    if "nc" not in _NC_CACHE:
        _NC_CACHE["nc"] = build()
    nc = _NC_CACHE["nc"]
    res = run_bass_kernel_spmd(nc, maps, core_ids=list(range(8)))
    outs = [np.asarray(res.results[c]["out"], np.float32) for c in range(8)]
    full = np.concatenate(outs, axis=0).reshape(2, 4096, D)
    return full
```
